# Optimizing a Trainium2 kernel written in Bass

```python
import math
import jax, jax.numpy as jnp
from jax import lax
import numpy as np

D_MODEL = 1024
BATCH = 4
SEQ = 4096
DEPTH = 1

MEM_LEN = 256
FOX_HEAD_DIM = 64
FOX_WIDTH = D_MODEL // 2
FOX_HEADS = FOX_WIDTH // FOX_HEAD_DIM
HGRN_KDIM = 128
HGRN_VDIM = 128
HGRN_WIDTH = D_MODEL - FOX_WIDTH
HGRN_HEADS = HGRN_WIDTH // HGRN_VDIM
IN_COLS = 3 * FOX_WIDTH + FOX_HEADS + 4 * HGRN_WIDTH
XATTN_HEADS = 4
XATTN_HEAD_DIM = D_MODEL // XATTN_HEADS
D_FF = 2816
CONV_WIDTH = 3
Q_BLOCK = 128
HGRN_CHUNK = 64
LN_EPS = 1e-5
RMS_EPS = 1e-6
ALPHA = (2.0 * DEPTH) ** 0.25
BETA = (8.0 * DEPTH) ** -0.25

kernel_name = 'hymba_fox_hgrn2_convffn_deepnorm'


def _layer_norm(x, g, b):
    xf = x.astype(jnp.float32)
    mu = jnp.mean(xf, axis=-1, keepdims=True)
    var = jnp.mean(jnp.square(xf - mu), axis=-1, keepdims=True)
    y = (xf - mu) * lax.rsqrt(var + LN_EPS)
    return (y * g.astype(jnp.float32) + b.astype(jnp.float32)).astype(x.dtype)


def _heads(t, n, d):
    b, s, _ = t.shape
    return t.reshape(b, s, n, d).transpose(0, 2, 1, 3)


def _merge(t):
    b, n, s, d = t.shape
    return t.transpose(0, 2, 1, 3).reshape(b, s, n * d)


def _forgetting_attention(q, k, v, log_f):
    b, h, s, d = q.shape
    nb = s // Q_BLOCK
    scale = 1.0 / math.sqrt(d)
    c = jnp.cumsum(log_f, axis=-1)
    qb = q.reshape(b, h, nb, Q_BLOCK, d).transpose(2, 0, 1, 3, 4)
    cb = c.reshape(b, h, nb, Q_BLOCK).transpose(2, 0, 1, 3)
    key_pos = jnp.arange(s)

    def block(args):
        q_blk, c_blk, idx = args
        logits = jnp.einsum('bhqd,bhkd->bhqk', q_blk, k).astype(jnp.float32) * scale
        logits = logits + c_blk[..., :, None] - c[:, :, None, :]
        q_pos = idx * Q_BLOCK + jnp.arange(Q_BLOCK)
        causal = key_pos[None, :] <= q_pos[:, None]
        logits = jnp.where(causal, logits, -jnp.inf)
        p = jax.nn.softmax(logits, axis=-1)
        return jnp.einsum('bhqk,bhkd->bhqd', p.astype(v.dtype), v)

    out = lax.map(block, (qb, cb, jnp.arange(nb)))
    return out.transpose(1, 2, 0, 3, 4).reshape(b, h, s, d)


def _hgrn2_chunkwise(q, k, v, log_f):
    b, h, s, dk = q.shape
    dv = v.shape[-1]
    n = s // HGRN_CHUNK

    def chunks(t):
        return t.reshape(b, h, n, HGRN_CHUNK, t.shape[-1]).transpose(2, 0, 1, 3, 4)

    tri = jnp.tril(jnp.ones((HGRN_CHUNK, HGRN_CHUNK), dtype=bool))

    def step(state, inp):
        qc, kc, vc, lfc = inp
        cum = jnp.cumsum(lfc, axis=-2)
        diff = cum[:, :, :, None, :] - cum[:, :, None, :, :]
        decay = jnp.exp(jnp.where(tri[:, :, None], diff, -jnp.inf))
        scores = jnp.einsum('bhtd,bhsd,bhtsd->bhts', qc, kc, decay)
        o_intra = jnp.einsum('bhts,bhsv->bhtv', scores, vc)
        o_inter = jnp.einsum('bhtd,bhdv->bhtv', qc * jnp.exp(cum), state)
        last = cum[:, :, -1, :]
        k_dec = kc * jnp.exp(last[:, :, None, :] - cum)
        new_state = jnp.exp(last)[..., None] * state + jnp.einsum('bhsd,bhsv->bhdv', k_dec, vc)
        return new_state, o_intra + o_inter

    state0 = jnp.zeros((b, h, dk, dv), jnp.float32)
    _, out = lax.scan(step, state0, (chunks(q), chunks(k), chunks(v), chunks(log_f)))
    return out.transpose(1, 2, 0, 3, 4).reshape(b, h, s, dv)


def _hybrid_mixer(x, w_in, b_fox_f, lb, norm_g, w_out):
    dt = x.dtype
    proj = jnp.einsum('bsd,dc->bsc', x, w_in)
    sizes = [FOX_WIDTH] * 3 + [FOX_HEADS] + [HGRN_WIDTH] * 4
    splits = np.cumsum(sizes)[:-1].tolist()
    fq, fk, fv, ff, hq, hf, hi, hg = jnp.split(proj, splits, axis=-1)

    fox_logf = jax.nn.log_sigmoid((ff + b_fox_f).astype(jnp.float32)).transpose(0, 2, 1)
    fox_o = _forgetting_attention(_heads(fq, FOX_HEADS, FOX_HEAD_DIM),
                                  _heads(fk, FOX_HEADS, FOX_HEAD_DIM),
                                  _heads(fv, FOX_HEADS, FOX_HEAD_DIM), fox_logf)
    fox_o = _merge(fox_o).astype(dt)

    z = hf.astype(jnp.float32)
    lbf = lb.astype(jnp.float32)
    f = lbf + (1.0 - lbf) * jax.nn.sigmoid(z)
    log_f = jnp.log(f)
    k_in = (1.0 - lbf) * jax.nn.sigmoid(-z)
    o = _hgrn2_chunkwise(_heads(hq.astype(jnp.float32), HGRN_HEADS, HGRN_KDIM),
                         _heads(k_in, HGRN_HEADS, HGRN_KDIM),
                         _heads(hi.astype(jnp.float32), HGRN_HEADS, HGRN_VDIM),
                         _heads(log_f, HGRN_HEADS, HGRN_KDIM))
    o = o * lax.rsqrt(jnp.mean(jnp.square(o), axis=-1, keepdims=True) + RMS_EPS)
    o = o * norm_g.astype(jnp.float32).reshape(1, HGRN_HEADS, 1, HGRN_VDIM)
    hgrn_o = (_merge(o) * jax.nn.silu(hg.astype(jnp.float32))).astype(dt)

    mixed = jnp.concatenate([fox_o, hgrn_o], axis=-1)
    return jnp.einsum('bsc,cd->bsd', mixed, w_out)


def _memory_cross_attention(x, mem, wq, wkv, wo):
    q = _heads(jnp.einsum('bsd,dc->bsc', x, wq), XATTN_HEADS, XATTN_HEAD_DIM)
    kv = jnp.einsum('bmd,dc->bmc', mem, wkv)
    k, v = jnp.split(kv, 2, axis=-1)
    k = _heads(k, XATTN_HEADS, XATTN_HEAD_DIM)
    v = _heads(v, XATTN_HEADS, XATTN_HEAD_DIM)
    logits = jnp.einsum('bhqd,bhmd->bhqm', q, k).astype(jnp.float32) / math.sqrt(XATTN_HEAD_DIM)
    p = jax.nn.softmax(logits, axis=-1)
    o = jnp.einsum('bhqm,bhmd->bhqd', p.astype(v.dtype), v)
    return jnp.einsum('bsc,cd->bsd', _merge(o), wo)


def _conv_ffn(x, w_up, conv_w, conv_b, w_down):
    u = jnp.einsum('bsd,df->bsf', x, w_up)
    ch = u.shape[-1]
    u = lax.conv_general_dilated(u, conv_w.reshape(CONV_WIDTH, 1, ch).astype(u.dtype),
                                 window_strides=(1,), padding=[(CONV_WIDTH - 1, 0)],
                                 dimension_numbers=('NWC', 'WIO', 'NWC'),
                                 feature_group_count=ch) + conv_b
    a, g = jnp.split(u, 2, axis=-1)
    return jnp.einsum('bsf,fd->bsd', jax.nn.silu(g) * a, w_down)


def setup_inputs(seed: int = 0) -> dict:
    key = jax.random.key(seed)
    ks = jax.random.split(key, 24)
    f32 = jnp.float32
    nrm = lambda k, shp, s: jax.random.normal(k, shp, f32) * s
    d_is = D_MODEL ** -0.5
    col_scale = jnp.concatenate([
        jnp.ones((2 * FOX_WIDTH,), f32), jnp.full((FOX_WIDTH,), BETA, f32),
        jnp.ones((FOX_HEADS + 2 * HGRN_WIDTH,), f32), jnp.full((HGRN_WIDTH,), BETA, f32),
        jnp.ones((HGRN_WIDTH,), f32)])
    kv_scale = jnp.concatenate([jnp.ones((D_MODEL,), f32), jnp.full((D_MODEL,), BETA, f32)])
    return {
        'x': nrm(ks[0], (BATCH, SEQ, D_MODEL), 1.0),
        'mem': nrm(ks[1], (BATCH, MEM_LEN, D_MODEL), 1.0),
        'w_in': nrm(ks[2], (DEPTH, D_MODEL, IN_COLS), d_is) * col_scale,
        'b_fox_f': jax.random.uniform(ks[3], (DEPTH, FOX_HEADS), f32, 1.0, 4.0),
        'hgrn_lb_logits': nrm(ks[4], (DEPTH + 1, HGRN_WIDTH), 0.5),
        'hgrn_norm_g': 1.0 + nrm(ks[5], (DEPTH, HGRN_WIDTH), 0.02),
        'w_out': nrm(ks[6], (DEPTH, D_MODEL, D_MODEL), d_is * BETA),
        'ln1_g': 1.0 + nrm(ks[7], (DEPTH, D_MODEL), 0.02),
        'ln1_b': nrm(ks[8], (DEPTH, D_MODEL), 0.02),
        'xa_wq': nrm(ks[9], (DEPTH, D_MODEL, D_MODEL), d_is),
        'xa_wkv': nrm(ks[10], (DEPTH, D_MODEL, 2 * D_MODEL), d_is) * kv_scale,
        'xa_wo': nrm(ks[11], (DEPTH, D_MODEL, D_MODEL), d_is * BETA),
        'ln2_g': 1.0 + nrm(ks[12], (DEPTH, D_MODEL), 0.02),
        'ln2_b': nrm(ks[13], (DEPTH, D_MODEL), 0.02),
        'ffn_w_up': nrm(ks[14], (DEPTH, D_MODEL, 2 * D_FF), d_is),
        'ffn_conv_w': nrm(ks[15], (DEPTH, CONV_WIDTH, 2 * D_FF), CONV_WIDTH ** -0.5),
        'ffn_conv_b': nrm(ks[16], (DEPTH, 2 * D_FF), 0.02),
        'ffn_w_down': nrm(ks[17], (DEPTH, D_FF, D_MODEL), D_FF ** -0.5 * BETA),
        'ln3_g': 1.0 + nrm(ks[18], (DEPTH, D_MODEL), 0.02),
        'ln3_b': nrm(ks[19], (DEPTH, D_MODEL), 0.02),
    }


def reference(x, mem, w_in, b_fox_f, hgrn_lb_logits, hgrn_norm_g, w_out, ln1_g, ln1_b,
              xa_wq, xa_wkv, xa_wo, ln2_g, ln2_b, ffn_w_up, ffn_conv_w, ffn_conv_b,
              ffn_w_down, ln3_g, ln3_b):
    lb_table = jnp.cumsum(jax.nn.softmax(hgrn_lb_logits.astype(jnp.float32), axis=0), axis=0)
    for l in range(DEPTH):
        mix = _hybrid_mixer(x, w_in[l], b_fox_f[l], lb_table[l], hgrn_norm_g[l], w_out[l])
        x = _layer_norm(ALPHA * x + mix, ln1_g[l], ln1_b[l])
        xa = _memory_cross_attention(x, mem, xa_wq[l], xa_wkv[l], xa_wo[l])
        x = _layer_norm(ALPHA * x + xa, ln2_g[l], ln2_b[l])
        ff = _conv_ffn(x, ffn_w_up[l], ffn_conv_w[l], ffn_conv_b[l], ffn_w_down[l])
        x = _layer_norm(ALPHA * x + ff, ln3_g[l], ln3_b[l])
    return x
```

```python
import os
from contextlib import ExitStack
import numpy as np
import concourse.bass as bass
import concourse.mybir as mybir
from concourse.bass_utils import run_bass_kernel_spmd

F32 = mybir.dt.float32
BF16 = mybir.dt.bfloat16
AF = mybir.ActivationFunctionType
ALU = mybir.AluOpType

D = 1024
NCH = 8
TCTX = 4096
P0 = 1920
TP = 2176
QB = [(0, 128), (128, 512), (640, 512), (1152, 512), (1664, 512)]
DFF = 2816
NFT = 22
ALPHA = 2.0 ** 0.25
LN_EPS = 1e-5
RMS_EPS = 1e-6
ENGS = ('pe', 'act', 'dve', 'pool', 'sp')

PP_LN = 0
PP_LBL = 48
PP_NG = 56
PP_BFF = 60
PP_HALO = 61
PP_CB = 62
PP_CW = 106
PP_N = 240
C_TRI = 0
C_HM = 128
C_ID = 256
C_SBE = 384
C_SBO = 512
C_SQ = 640
C_SK = 640 + 576
C_N = 640 + 1152


class Sched:
    def __init__(self, nc, es):
        self.nc = nc
        self.es = es
        self.sem = {e: es.enter_context(nc.semaphore("s_" + e)) for e in ENGS}
        self.cnt = {e: 0 for e in ENGS}
        self.dsem = {}
        self.dcnt = {}
        self.prog = {e: [] for e in ENGS}
        self.waited = {e: {} for e in ENGS}
        self.lastw = {}
        self.readers = {}
        self.nps = 0
        self.tog = 0

    def _deps(self, eng, r, w):
        deps = {}

        def need(c):
            if c is None:
                return
            k, v = c
            if deps.get(k, 0) < v:
                deps[k] = v
        for key in r:
            need(self.lastw.get(key))
        for key in w:
            need(self.lastw.get(key))
            for c in self.readers.get(key, ()):
                need(c)
        waits = []
        for k, v in deps.items():
            if k == eng and eng == 'pe':
                continue
            if self.waited[eng].get(k, 0) >= v:
                continue
            self.waited[eng][k] = v
            waits.append((k, v))
        return waits

    def _mark(self, comp, r, w):
        for key in r:
            self.readers.setdefault(key, []).append(comp)
        for key in w:
            self.lastw[key] = comp
            self.readers[key] = []

    def op(self, eng, fn, r=(), w=()):
        waits = self._deps(eng, r, w)
        self.cnt[eng] += 1
        comp = (eng, self.cnt[eng])
        self.prog[eng].append((waits, fn, (eng, 1)))
        self._mark(comp, r, w)

    def dma(self, eng, fns, skey, r=(), w=()):
        if skey not in self.dsem:
            self.dsem[skey] = self.es.enter_context(self.nc.semaphore("d_%d" % len(self.dsem)))
            self.dcnt[skey] = 0
        waits = self._deps(eng, r, w)
        self.dcnt[skey] += 16 * len(fns)
        comp = (('d', skey), self.dcnt[skey])
        first = True
        for fn in fns:
            self.prog[eng].append((waits if first else [], fn, (('d', skey), 16)))
            first = False
        self._mark(comp, r, w)

    def _semof(self, k):
        return self.dsem[k[1]] if isinstance(k, tuple) else self.sem[k]

    def flush(self, final_waits=(), last=False):
        nc = self.nc
        with nc.Block(no_gpsimd_drain=not last) as block:
            def emit(name):
                def body(e):
                    for waits, fn, (k, amt) in self.prog[name]:
                        for wk, wv in waits:
                            e.wait_ge(self._semof(wk), wv)
                        fn(e).then_inc(self._semof(k), amt)
                    if name == 'sp':
                        for k in final_waits:
                            e.wait_ge(self.dsem[k], self.dcnt[k])
                return body
            block.tensor(emit('pe'))
            block.scalar(emit('act'))
            block.vector(emit('dve'))
            block.gpsimd(emit('pool'))
            block.sync(emit('sp'))
        for e in ENGS:
            self.prog[e] = []
            for k in ENGS:
                self.waited[e][k] = self.cnt[k]
        self.lastw = {key: c for key, c in self.lastw.items() if isinstance(c[0], tuple)}
        self.readers = {}


def build_program(debug=False):
    nc = bass.Bass("TRN2", target_bir_lowering=False)
    dt_in = lambda n, sh: nc.dram_tensor(n, sh, F32, kind="ExternalInput").ap()
    xT = dt_in("xT", [D, TCTX])
    memT = dt_in("memT", [D, 256])
    aux = dt_in("aux", [3, TCTX])
    pp_d = dt_in("pp", [128, PP_N])
    cst_d = dt_in("cst", [128, C_N])
    wqp = dt_in("wqp", [8, D, 72])
    wkp = dt_in("wkp", [8, D, 72])
    wffp = dt_in("wffp", [D, 128])
    w_in = dt_in("w_in", [D, 3592])
    w_out = dt_in("w_out", [D, D])
    xa_wq = dt_in("xa_wq", [D, D])
    xa_wkv = dt_in("xa_wkv", [D, 2 * D])
    xa_wo = dt_in("xa_wo", [D, D])
    w_up = dt_in("w_up", [D, 2 * DFF])
    w_down = dt_in("w_down", [DFF, D])
    outT = nc.dram_tensor("outT", [D, 2048], F32, kind="ExternalOutput").ap()
    x2s = nc.dram_tensor("x2s", [D, TP], F32).ap()
    mixs = nc.dram_tensor("mixs", [D, TP], BF16).ap()
    dbg = {}
    if debug:
        for n in ("d_mixed", "d_x1", "d_x2"):
            dbg[n] = nc.dram_tensor(n, [D, TP], F32, kind="ExternalOutput").ap()
        for n in ("d_os", "d_kt", "d_qt", "d_pt", "d_pb", "d_kt2"):
            dbg[n] = nc.dram_tensor(n, [128, 512], F32, kind="ExternalOutput").ap()

    with ExitStack() as es:
        S = Sched(nc, es)

        uid = [0]

        def T(n, sh, dt, stack=es):
            uid[0] += 1
            return stack.enter_context(nc.sbuf_tensor("sb%d_%s" % (uid[0], n), sh, dt))

        ps = [es.enter_context(nc.psum_tensor("ps%d" % i, [128, 512], F32)) for i in range(7)]
        psb = es.enter_context(nc.psum_tensor("psb", [128, 1024], BF16))

        psmod = [5]

        def PS():
            i = S.nps % psmod[0]
            S.nps += 1
            return ps[i], ('ps', i)

        npsa = [0]

        def PSA():
            i = 5 + npsa[0] % 2
            npsa[0] += 1
            return ps[i], ('ps', i)

        def MM(out, lhsT, rhs, start=True, stop=True, r=(), w=()):
            S.op('pe', lambda e: e.matmul(out, lhsT, rhs, start=start, stop=stop), r, w)

        def ACT(out, in_, func, r=(), w=(), bias=None, scale=None):
            kw = {}
            if bias is not None:
                kw['bias'] = bias
            if scale is not None:
                kw['scale'] = scale
            S.op('act', lambda e: e.activation(out=out, in_=in_, func=func, **kw), r, w)

        def EVAC(out, in_, r=(), w=(), eng=None):
            if eng is None:
                S.tog ^= 1
                eng = 'act' if S.tog else 'dve'
            if eng == 'act':
                S.op('act', lambda e: e.activation(out=out, in_=in_, func=AF.Copy), r, w)
            else:
                S.op(eng, lambda e: e.tensor_copy(out=out, in_=in_), r, w)

        def TT(eng, out, in0, in1, op, r=(), w=()):
            S.op(eng, lambda e: e.tensor_tensor(out=out, in0=in0, in1=in1, op=op), r, w)

        def TS(eng, out, in0, s1, s2, op0, op1=None, r=(), w=()):
            if op1 is None:
                S.op(eng, lambda e: e.tensor_scalar(out=out, in0=in0, scalar1=s1, scalar2=None, op0=op0), r, w)
            else:
                S.op(eng, lambda e: e.tensor_scalar(out=out, in0=in0, scalar1=s1, scalar2=s2, op0=op0, op1=op1), r, w)

        def STT(eng, out, in0, scalar, in1, op0, op1, r=(), w=()):
            S.op(eng, lambda e: e.scalar_tensor_tensor(out=out, in0=in0, scalar=scalar, in1=in1, op0=op0, op1=op1), r, w)

        def MEMSET(eng, ap, val, w=()):
            S.op(eng, lambda e: e.memset(ap, val), (), w)

        def DMA(eng, out, in_, skey, r=(), w=()):
            S.dma(eng, [lambda e: e.dma_start(out=out, in_=in_)], skey, r, w)

        def slab(src, c0, n):
            return src[:, c0:c0 + n].rearrange("(c p) n -> p c n", p=128)

        dbt = T("dbt", [128, 512], F32) if debug else None

        def DBG(name, ap, keys):
            if not debug:
                return
            EVAC(dbt[:], ap, r=list(keys) + ['dbt_dma'], w=['dbt'], eng='dve')
            DMA('sp', dbg[name], dbt[:], 'dbg2', r=['dbt'], w=['dbt_dma'])

        pp = T("pp", [128, PP_N], F32)
        cst = T("cst", [128, C_N], BF16)
        ones_d = T("ones_d", [128, 128], BF16)
        ones_v = T("ones_v", [128, 128], BF16)
        ones_1 = T("ones_1", [128, 128], BF16)
        lbt = T("lbt", [128, 16], F32)
        nbf = T("nbf", [128, 1], F32)

        DMA('sp', pp[:], pp_d, 'pp', w=['pp'])
        DMA('pool', cst[:], cst_d, 'cst', w=['cst'])
        MEMSET('dve', ones_d[:], 1.0 / 1024.0, w=['ones_d'])
        MEMSET('dve', ones_v[:], 1.0 / 128.0, w=['ones_v'])
        MEMSET('dve', ones_1[:], 1.0, w=['ones_1'])
        ACT(lbt[:, 8:16], pp[:, PP_LBL:PP_LBL + 8], AF.Exp, r=['pp'], w=['lbt'])
        TT('dve', lbt[:, 4:8], lbt[:, 8:12], lbt[:, 12:16], ALU.add, r=['lbt'], w=['lbt'])
        S.op('dve', lambda e: e.reciprocal(out=lbt[:, 4:8], in_=lbt[:, 4:8]), ['lbt'], ['lbt'])
        TT('dve', lbt[:, 0:4], lbt[:, 8:12], lbt[:, 4:8], ALU.mult, r=['lbt'], w=['lbt'])
        TS('dve', lbt[:, 4:8], lbt[:, 0:4], -1.0, 1.0, ALU.mult, ALU.add, r=['lbt'], w=['lbt'])
        TS('dve', lbt[:, 8:12], lbt[:, 4:8], -1.0, None, ALU.mult, r=['lbt'], w=['lbt'])
        TS('dve', nbf[:], pp[:, PP_BFF:PP_BFF + 1], -1.0, None, ALU.mult, r=['pp'], w=['nbf'])

        with ExitStack() as es_mix:
            mixedT = T("mixedT", [128, NCH, TP], BF16, es_mix)
            xTb = T("xTb", [128, NCH, TCTX], BF16, es_mix)
            for hf in range(2):
                for c in range(NCH):
                    DMA('pool', xTb[:, c, hf * 2048:(hf + 1) * 2048], xT[c * 128:(c + 1) * 128, hf * 2048:(hf + 1) * 2048],
                        ('xT', c, hf), w=[('xT', c, hf)])
            XK = [('xT', c, hf) for c in range(NCH) for hf in range(2)]
            augk = T("augk", [128, TCTX], BF16, es_mix)
            augq = T("augq", [128, TP], BF16, es_mix)

            with ExitStack() as es_c:
                wff = T("wff", [128, NCH, 128], BF16, es_c)
                sp_t = T("sp_t", [128, TCTX], F32, es_c)
                cs_t = T("cs_t", [128, TCTX], F32, es_c)
                hb = T("hb", [128, TCTX], BF16, es_c)
                one_f = T("one_f", [128, 512], F32, es_c)
                DMA('pool', wff[:], slab(wffp, 0, 128), 'wff', w=['wff'])
                MEMSET('dve', one_f[:], 1.0, w=['one_f'])
                MEMSET('dve', augk[:], 0.0, w=['augk'])
                MEMSET('dve', augq[:], 0.0, w=['augq'])
                for blk in range(8):
                    cs = slice(blk * 512, (blk + 1) * 512)
                    p, pk = PS()
                    for c in range(NCH):
                        MM(p[:, :], wff[:, c, :], xTb[:, c, cs], start=(c == 0), stop=(c == NCH - 1),
                           r=['wff', ('xT', c, blk // 4)], w=[pk])
                    ACT(sp_t[:, cs], p[:, :], AF.Exp, r=[pk, 'nbf'], w=[('sp', blk)], bias=nbf[:, 0:1], scale=-1.0)
                    ACT(sp_t[:, cs], sp_t[:, cs], AF.Ln, r=[('sp', blk)], w=[('sp', blk)], bias=1.0)
                    init = 0.0 if blk == 0 else cs_t[:, blk * 512 - 1:blk * 512]

                    def scan(e, cs=cs, init=init):
                        return e.tensor_tensor_scan(out=cs_t[:, cs], data0=one_f[:], data1=sp_t[:, cs], initial=init,
                                                    op0=ALU.mult, op1=ALU.add)
                    S.op('dve', scan, [('sp', blk), 'one_f', 'cs'], ['cs'])
                for hf in range(2):
                    cs = slice(hf * 2048, (hf + 1) * 2048)
                    qlo = max(hf * 2048, P0)
                    qs_ = slice(qlo, (hf + 1) * 2048)
                    qd = slice(qlo - P0, (hf + 1) * 2048 - P0)
                    for part, rows in enumerate((slice(0, 32), slice(32, 64), slice(64, 96))):
                        EVAC(hb[:, cs], cs_t[:, cs], r=['cs'], w=['hb'], eng='dve')
                        EVAC(augk[rows, cs], hb[rows, cs], r=['hb', 'augk'], w=['augk'], eng='dve')
                        TS('dve', augq[rows, qd], hb[rows, qs_], -8.0, None, ALU.mult, r=['hb', 'augq'], w=['augq'])
                        if part < 2:
                            TT('dve', cs_t[:, cs], cs_t[:, cs], hb[:, cs], ALU.subtract, r=['hb', 'cs'], w=['cs'])
                DMA('pool', augk[96:98, :], aux[0:2, :], 'auxk', r=['augk'], w=['augk'])
                DMA('pool', augq[96:97, :], aux[2:3, 0:TP], 'auxq', r=['augq'], w=['augq'])
                S.flush()

            with ExitStack() as es_f:
                KT = [T("KT%d" % i, [128, TCTX], BF16, es_f) for i in range(2)]
                QT = [T("QT%d" % i, [128, TP], BF16, es_f) for i in range(2)]
                V1 = [T("V1%d" % i, [128, 32, 128], BF16, es_f) for i in range(2)]
                WQ = [T("WQ%d" % i, [128, NCH, 72], BF16, es_f) for i in range(2)]
                WK = [T("WK%d" % i, [128, NCH, 72], BF16, es_f) for i in range(2)]
                WV = [T("WV%d" % i, [128, NCH, 64], BF16, es_f) for i in range(2)]
                NPT = 6
                PT = [T("PT%d" % i, [128, 512], BF16, es_f) for i in range(NPT)]
                OS = [[T("OS%d_%d" % (p_, i), [128, 512], F32, es_f) for i in range(2)] for p_ in range(2)]
                RR = [[T("RR%d_%d" % (p_, i), [128, 512], BF16, es_f) for i in range(2)] for p_ in range(2)]
                for par in range(2):
                    MEMSET('dve', V1[par][:], 0.0, w=[('V1', par)])
                    for i in range(2):
                        MEMSET('dve', RR[par][i][:], 0.0, w=[('RR', par, i)])
                MEMSET('dve', V1[0][:, :, 64:65], 1.0, w=[('V1', 0)])
                MEMSET('dve', V1[1][:, :, 0:1], 1.0, w=[('V1', 1)])

                def proj_gen(h):
                    par = h % 2
                    voff = 0 if par == 0 else 64
                    DMA('pool', WQ[par][:], wqp[h].rearrange("(c p) n -> p c n", p=128), ('WQ', par), w=[('WQ', par)])
                    DMA('pool', WK[par][:], wkp[h].rearrange("(c p) n -> p c n", p=128), ('WK', par), w=[('WK', par)])
                    DMA('pool', WV[par][:], slab(w_in, 1024 + h * 64, 64), ('WV', par), w=[('WV', par)])
                    selq = cst[:, C_SQ + h * 72:C_SQ + h * 72 + 71]
                    selk = cst[:, C_SK + h * 72:C_SK + h * 72 + 71]
                    yield
                    for blk in range(8):
                        cs = slice(blk * 512, (blk + 1) * 512)
                        p, pk = PS()
                        for c in range(NCH):
                            MM(p[0:71, :], WK[par][:, c, 0:71], xTb[:, c, cs], start=(c == 0), stop=False,
                               r=[('WK', par)], w=[pk])
                        MM(p[0:71, :], selk, augk[:, cs], start=False, stop=True, r=['cst', 'augk'], w=[pk])
                        EVAC(KT[par][0:71, cs], p[0:71, :], r=[pk], w=[('KT', par)], eng='dve')
                        yield
                    for (qs, nq) in QB:
                        cs = slice(P0 + qs, P0 + qs + nq)
                        p, pk = PS()
                        for c in range(NCH):
                            MM(p[0:71, 0:nq], WQ[par][:, c, 0:71], xTb[:, c, cs], start=(c == 0), stop=False,
                               r=[('WQ', par)], w=[pk])
                        MM(p[0:71, 0:nq], selq, augq[:, qs:qs + nq], start=False, stop=True, r=['cst', 'augq'], w=[pk])
                        EVAC(QT[par][0:71, qs:qs + nq], p[0:71, 0:nq], r=[pk], w=[('QT', par)], eng='dve')
                        yield
                    for g in range(8):
                        p, pk = PS()
                        for j in range(4):
                            tl = 4 * g + j
                            for c in range(NCH):
                                MM(p[:, j * 64:(j + 1) * 64], xTb[:, c, tl * 128:(tl + 1) * 128], WV[par][:, c, :],
                                   start=(c == 0), stop=(c == NCH - 1), r=[('WV', par)], w=[pk])
                        EVAC(V1[par][:, 4 * g:4 * g + 4, voff:voff + 64],
                             p[:, 0:256].rearrange("p (j n) -> p j n", n=64), r=[pk], w=[('V1', par)], eng='dve')
                        yield

                npt = 0
                nfin = [0, 0]
                for _ in proj_gen(0):
                    pass
                deferred = []
                for h in range(8):
                    par = h % 2
                    nxt = proj_gen(h + 1) if h + 1 < 8 else None
                    for (qs, nq) in QB:
                        qctx = P0 + qs
                        nkt = (qctx + nq) // 128
                        diag = [kt for kt in range(nkt) if kt * 128 >= qctx]
                        full = [kt for kt in range(nkt) if kt * 128 < qctx]
                        order = [full[0]] + diag + full[1:]
                        po, pok = PSA()
                        orows = slice(0, 65) if par == 0 else slice(0, 128)
                        LOOK = 3
                        pend = []
                        for step in range(len(order) + LOOK):
                            if step < len(order):
                                kt = order[step]
                                ks = kt * 128
                                off = max(0, ks - qctx)
                                n = nq - off
                                pS, pSk = PS()
                                MM(pS[:, 0:n], KT[par][0:71, ks:ks + 128], QT[par][0:71, qs + off:qs + nq],
                                   r=[('KT', par), ('QT', par)], w=[pSk])
                                pt = PT[npt % NPT]
                                ptk = ('PT', npt % NPT)
                                npt += 1
                                ACT(pt[:, 0:n], pS[:, 0:n], AF.Exp, r=[pSk], w=[ptk], scale=0.125)
                                if ks >= qctx:
                                    TT('pool', pt[:, 0:128], pt[:, 0:128], cst[:, C_TRI:C_TRI + 128], ALU.mult,
                                       r=[ptk, 'cst'], w=[ptk])
                                pend.append((pt, ptk, kt, off, n))
                            if step >= LOOK:
                                idx = step - LOOK
                                pt, ptk, kt, off, n = pend[idx]
                                MM(po[orows, off:nq], V1[par][:, kt, orows], pt[:, 0:n], start=(idx == 0),
                                   stop=(idx == len(order) - 1), r=[ptk, ('V1', par)], w=[pok])
                            if step == 5:
                                for d_ in deferred:
                                    d_()
                                deferred = []
                            if nxt is not None and step % 4 == 3:
                                next(nxt, None)
                        fi = nfin[par] % 2
                        nfin[par] += 1
                        osb = OS[par][fi]
                        osk = ('OS', par, fi)
                        rr = RR[par][fi]
                        rrk = ('RR', par, fi)
                        EVAC(osb[orows, 0:nq], po[orows, 0:nq], r=[pok], w=[osk], eng='act')
                        dr = 64 if par == 0 else 0
                        TS('dve', osb[dr:dr + 1, 0:nq], osb[dr:dr + 1, 0:nq], 1e-30, None, ALU.max, r=[osk], w=[osk])
                        S.op('dve', lambda e, osb=osb, dr=dr, nq=nq: e.reciprocal(
                            out=osb[dr:dr + 1, 0:nq], in_=osb[dr:dr + 1, 0:nq]), [osk], [osk])
                        EVAC(rr[dr:dr + 1, 0:nq], osb[dr:dr + 1, 0:nq], r=[osk], w=[rrk], eng='dve')

                        def fin2(h=h, par=par, qs=qs, nq=nq, osb=osb, osk=osk, rr=rr, rrk=rrk):
                            pb, pbk = PS()
                            if par == 0:
                                MM(pb[0:64, 0:nq], cst[0:65, C_SBE:C_SBE + 64], rr[0:65, 0:nq], r=[rrk, 'cst'], w=[pbk])
                                vr = slice(0, 64)
                            else:
                                MM(pb[:, 0:nq], cst[0:1, C_SBO:C_SBO + 128], rr[0:1, 0:nq], r=[rrk, 'cst'], w=[pbk])
                                vr = slice(64, 128)
                            TT('dve', mixedT[vr, h // 2, qs:qs + nq], osb[vr, 0:nq], pb[vr, 0:nq], ALU.mult,
                               r=[osk, pbk], w=[('mix', h // 2)])
                        deferred.append(fin2)
                    if nxt is not None:
                        for _ in nxt:
                            pass
                for d_ in deferred:
                    d_()
                for c in range(4):
                    DMA('sp', mixs[c * 128:(c + 1) * 128, :], mixedT[:, c, :], 'mixs', r=[('mix', c)], w=[('mixs', c)])
                S.flush()

            with ExitStack() as es_h:
                rmask = T("rmask", [128, 512], F32, es_h)
                MEMSET('dve', rmask[:], 1.0, w=['rmask'])
                MEMSET('dve', rmask[:].rearrange("p (c s) -> p c s", s=64)[:, :, 0:1], 0.0, w=['rmask'])
                free_banks = [0, 1, 2]

                def PSalloc():
                    assert free_banks, "out of general PSUM banks"
                    i = free_banks.pop(0)
                    return ps[i], ('ps', i), i

                def PSfree(i):
                    free_banks.append(i)

                def alloc_set(si):
                    Bf = {}
                    for n_ in ('WHQ', 'WHF', 'WHI', 'WHG'):
                        Bf[n_] = T("%s%d" % (n_, si), [128, NCH, 128], BF16, es_h)
                    for n_ in ('sig', 'lf', 'cum', 'kf', 'dl', 'sg', 'aa', 'ea', 'osb', 'qf'):
                        Bf[n_] = T("%s%d" % (n_, si), [128, 512], F32, es_h)
                    for n_ in ('kdec', 'qt', 'ktl', 'qe', 'stb', 'osq'):
                        Bf[n_] = T("%s%d" % (n_, si), [128, 512], BF16, es_h)
                    for n_ in ('kdT', 'vtok'):
                        Bf[n_] = T("%s%d" % (n_, si), [128, 4, 128], BF16, es_h)
                    Bf['elast'] = T("elast%d" % si, [128, 8], F32, es_h)
                    Bf['Sst'] = [T("Sst%d_%d" % (si, i), [128, 128], F32, es_h) for i in range(2)]
                    Bf['Sbf'] = [T("Sbf%d_%d" % (si, i), [128, 128], BF16, es_h) for i in range(2)]
                    return Bf
                hsets = [alloc_set(0), alloc_set(1)]

                def head_gen(h, si):
                    Bf = hsets[si]
                    K_ = lambda n_: ('h', si, n_)
                    pstb = ps[3 + si]
                    pob = ps[5 + si]
                    pok = ('ps', 5 + si)
                    psbh = psb[:, si * 512:(si + 1) * 512]
                    psbk = 'psb'
                    for n_, c0 in (('WHQ', 1544), ('WHF', 2056), ('WHI', 2568), ('WHG', 3080)):
                        DMA('pool', Bf[n_][:], slab(w_in, c0 + h * 128, 128), K_(n_), w=[K_(n_)])
                    lb_c = lbt[:, h:h + 1]
                    oml_c = lbt[:, 4 + h:5 + h]
                    noml_c = lbt[:, 8 + h:9 + h]
                    sig, lf, cum, kf, dl, sg, aa, ea, osb, qf = (Bf[n_] for n_ in ('sig', 'lf', 'cum', 'kf', 'dl', 'sg', 'aa', 'ea', 'osb', 'qf'))
                    kdec, qt, ktl, qe, stb, osq, kdT, vtok, elast = (Bf[n_] for n_ in ('kdec', 'qt', 'ktl', 'qe', 'stb', 'osq', 'kdT', 'vtok', 'elast'))
                    Sst, Sbf = Bf['Sst'], Bf['Sbf']
                    MEMSET('dve', Sst[0][:], 0.0, w=[K_(('Sst', 0))])
                    MEMSET('dve', Sbf[0][:], 0.0, w=[K_(('Sbf', 0))])
                    nst = 0
                    yield
                    for blk in range(8):
                        cs = slice(blk * 512, (blk + 1) * 512)
                        if blk == 3:
                            pr = (384, 128)
                        elif blk >= 4:
                            pr = (0, 512)
                        else:
                            pr = None
                        p, pk, pi = PSalloc()
                        for c in range(NCH):
                            MM(p[:, :], Bf['WHF'][:, c, :], xTb[:, c, cs], start=(c == 0), stop=(c == NCH - 1), r=[K_('WHF')], w=[pk])
                        ACT(sig[:], p[:, :], AF.Sigmoid, r=[pk], w=[K_('sig')])
                        PSfree(pi)
                        if pr is not None:
                            o0, n = pr
                            ws = slice(o0, o0 + n)
                            xcs = slice(blk * 512 + o0, blk * 512 + o0 + n)
                            pg, pgk, pgi = PSalloc()
                            for c in range(NCH):
                                MM(pg[:, 0:n], Bf['WHG'][:, c, :], xTb[:, c, xcs], start=(c == 0), stop=(c == NCH - 1), r=[K_('WHG')], w=[pgk])
                            ACT(sg[:, ws], pg[:, 0:n], AF.Silu, r=[pgk], w=[K_('sg')])
                            PSfree(pgi)
                        yield
                        p2, p2k, p2i = PSalloc()
                        for j in range(4):
                            tl = blk * 4 + j
                            for c in range(NCH):
                                MM(p2[:, j * 128:(j + 1) * 128], xTb[:, c, tl * 128:(tl + 1) * 128], Bf['WHI'][:, c, :],
                                   start=(c == 0), stop=(c == NCH - 1), r=[K_('WHI')], w=[p2k])
                        EVAC(vtok[:].rearrange("p j n -> p (j n)"), p2[:, :], r=[p2k], w=[K_('vtok')], eng='dve')
                        PSfree(p2i)
                        yield
                        if pr is not None:
                            o0, n = pr
                            ws = slice(o0, o0 + n)
                            xcs = slice(blk * 512 + o0, blk * 512 + o0 + n)
                            pq, pqk, pqi = PSalloc()
                            for c in range(NCH):
                                MM(pq[:, 0:n], Bf['WHQ'][:, c, :], xTb[:, c, xcs], start=(c == 0), stop=(c == NCH - 1), r=[K_('WHQ')], w=[pqk])
                            EVAC(qf[:, ws], pq[:, 0:n], r=[pqk], w=[K_('qf')], eng='dve')
                            PSfree(pqi)
                            yield
                        ACT(lf[:], sig[:], AF.Ln, r=[K_('sig'), 'lbt'], w=[K_('lf')], bias=lb_c, scale=oml_c)
                        TS('dve', kf[:], sig[:], noml_c, oml_c, ALU.mult, ALU.add, r=[K_('sig'), 'lbt'], w=[K_('kf')])
                        yield
                        S.op('dve', lambda e: e.tensor_tensor_scan(out=cum[:], data0=rmask[:], data1=lf[:], initial=0.0,
                                                                   op0=ALU.mult, op1=ALU.add), [K_('lf'), 'rmask'], [K_('cum')])
                        yield
                        cumv = cum[:].rearrange("p (c s) -> p c s", s=64)
                        TT('dve', dl[:].rearrange("p (c s) -> p c s", s=64), cumv[:, :, 63:64].broadcast_to([128, 8, 64]),
                           cumv, ALU.subtract, r=[K_('cum')], w=[K_('dl')])
                        ACT(elast[:], cumv[:, :, 63], AF.Exp, r=[K_('cum')], w=[K_('elast')])
                        yield
                        ACT(dl[:], dl[:], AF.Exp, r=[K_('dl')], w=[K_('dl')])
                        yield
                        TT('dve', kdec[:], kf[:], dl[:], ALU.mult, r=[K_('kf'), K_('dl')], w=[K_('kdec')])
                        yield
                        for j in range(4):
                            S.op('pe', lambda e, j=j: e.transpose(psbh[:, j * 128:(j + 1) * 128], kdec[:, j * 128:(j + 1) * 128],
                                                                  cst[:, C_ID:C_ID + 128]), [K_('kdec'), 'cst'], [psbk])
                        EVAC(kdT[:].rearrange("p j n -> p (j n)"), psbh, r=[psbk], w=[K_('kdT')], eng='act')
                        yield
                        if pr is not None:
                            o0, n = pr
                            ws = slice(o0, o0 + n)
                            xcs = slice(blk * 512 + o0, blk * 512 + o0 + n)
                            nc_ = n // 64
                            cumw = cum[:, ws].rearrange("p (c s) -> p c s", s=64)
                            TT('dve', aa[:, ws].rearrange("p (c s) -> p c s", s=64), cumw,
                               cumw[:, :, 31:32].broadcast_to([128, nc_, 64]), ALU.subtract, r=[K_('cum')], w=[K_('aa')])
                            yield
                            ACT(ea[:, ws], aa[:, ws], AF.Exp, r=[K_('aa')], w=[K_('ea')])
                            ACT(dl[:, ws], aa[:, ws], AF.Exp, r=[K_('aa'), K_('dl')], w=[K_('dl')], scale=-1.0)
                            yield
                            TT('dve', qt[:, ws], qf[:, ws], ea[:, ws], ALU.mult, r=[K_('qf'), K_('ea')], w=[K_('qt')])
                            TT('dve', ktl[:, ws], kf[:, ws], dl[:, ws], ALU.mult, r=[K_('kf'), K_('dl')], w=[K_('ktl')])
                            yield
                            ACT(ea[:, ws], cum[:, ws], AF.Exp, r=[K_('cum'), K_('ea')], w=[K_('ea')])
                            psc, psck, psci = PSalloc()
                            for j in range(n // 128):
                                c0 = o0 + j * 128
                                MM(psc[:, j * 128:(j + 1) * 128], ktl[:, c0:c0 + 128], qt[:, c0:c0 + 128], r=[K_('ktl'), K_('qt')], w=[psck])
                            yield
                            TT('dve', qe[:, ws], qf[:, ws], ea[:, ws], ALU.mult, r=[K_('qf'), K_('ea')], w=[K_('qe')])
                            hm = cst[:, C_HM:C_HM + 128]
                            TT('dve', stb[:, 0:n].rearrange("p (j t) -> p j t", t=128),
                               psc[:, 0:n].rearrange("p (j t) -> p j t", t=128),
                               hm.unsqueeze(1).broadcast_to([128, n // 128, 128]), ALU.mult, r=[psck, 'cst'], w=[K_('stb')])
                            PSfree(psci)
                            yield

                        def pst_ap(ch):
                            col = (si * 2 + (ch // 2) % 2) * 128
                            return ps[3 + ch % 2][:, col:col + 128]

                        def pst_mm(ch):
                            j = ch // 2
                            rows = slice((ch % 2) * 64, (ch % 2) * 64 + 64)
                            MM(pst_ap(ch), kdT[rows, j, :], vtok[rows, j, :],
                               r=[K_('kdT'), K_('vtok')], w=[('ps', 3 + ch % 2)])
                        for ch in range(4):
                            pst_mm(ch)
                        for ch in range(8):
                            j = ch // 2
                            half = ch % 2
                            col = ch * 64
                            g = blk * 8 + ch
                            processed = pr is not None and col >= pr[0]
                            if processed:
                                pc = col - pr[0]
                                if half == 0:
                                    MM(pob[:, pc:pc + 128], vtok[:, j, :], stb[:, pc:pc + 128], start=True, stop=False,
                                       r=[K_('vtok'), K_('stb')], w=[pok])
                                MM(pob[:, pc:pc + 64], Sbf[nst % 2][:, :], qe[:, col:col + 64], start=False, stop=(half == 1),
                                   r=[K_(('Sbf', nst % 2)), K_('qe')], w=[pok])
                            STT('dve', Sst[(nst + 1) % 2][:], Sst[nst % 2][:], elast[:, ch:ch + 1],
                                pst_ap(ch), ALU.mult, ALU.add,
                                r=[K_(('Sst', nst % 2)), K_('elast'), ('ps', 3 + ch % 2)], w=[K_(('Sst', (nst + 1) % 2))])
                            if ch + 4 < 8:
                                pst_mm(ch + 4)
                            nst += 1
                            if g + 1 >= P0 // 64 and g + 1 < 64:
                                EVAC(Sbf[nst % 2][:], Sst[nst % 2][:], r=[K_(('Sst', nst % 2))], w=[K_(('Sbf', nst % 2))], eng='act')
                            yield
                        if pr is not None:
                            o0, n = pr
                            ws = slice(o0, o0 + n)
                            EVAC(osb[:, 0:n], pob[:, 0:n], r=[pok], w=[K_('osb')], eng='dve')
                            yield
                            ACT(osq[:, 0:n], osb[:, 0:n], AF.Square, r=[K_('osb')], w=[K_('osq')])
                            yield
                            pss, pssk, pssi = PSalloc()
                            MM(pss[:, 0:n], ones_v[:], osq[:, 0:n], r=[K_('osq'), 'ones_v'], w=[pssk])
                            ACT(aa[:, 0:n], pss[:, 0:n], AF.Ln, r=[pssk, K_('aa')], w=[K_('aa')], bias=RMS_EPS)
                            PSfree(pssi)
                            yield
                            ACT(aa[:, 0:n], aa[:, 0:n], AF.Exp, r=[K_('aa')], w=[K_('aa')], scale=-0.5)
                            yield
                            STT('dve', osb[:, 0:n], osb[:, 0:n], pp[:, PP_NG + h:PP_NG + h + 1], aa[:, 0:n], ALU.mult, ALU.mult,
                                r=[K_('osb'), K_('aa'), 'pp'], w=[K_('osb')])
                            yield
                            t0 = blk * 512 + o0 - P0
                            TT('dve', mixedT[:, 4 + h, t0:t0 + n], osb[:, 0:n], sg[:, ws], ALU.mult,
                               r=[K_('osb'), K_('sg')], w=[('mix', 4 + h)])
                            yield

                for hp in range(2):
                    gens = [head_gen(2 * hp, 0), head_gen(2 * hp + 1, 1)]
                    if os.environ.get("HSEQ"):
                        for g_ in gens:
                            for _ in g_:
                                pass
                        gens = []
                    for _ in range(int(os.environ.get("HSKEW", "0"))):
                        next(gens[0])
                    while gens:
                        for g_ in list(gens):
                            try:
                                next(g_)
                            except StopIteration:
                                gens.remove(g_)
                    for c in (4 + 2 * hp, 5 + 2 * hp):
                        DMA('sp', mixs[c * 128:(c + 1) * 128, :], mixedT[:, c, :], 'mixs', r=[('mix', c)], w=[('mixs', c)])
                S.flush(final_waits=['mixs'])

        def layer_norm(Y, Ybf, gcol, bcol, es_l, blocks, yoff, post_block=None):
            sets = []
            for i in range(2):
                sets.append(dict(ybf=T("ln_ybf%d" % i, [128, NCH, 512], BF16, es_l), ysq=T("ln_ysq%d" % i, [128, NCH, 512], BF16, es_l),
                                 m2=T("ln_m2%d" % i, [128, 512], F32, es_l), rs=T("ln_rs%d" % i, [128, 512], F32, es_l),
                                 nm=T("ln_nm%d" % i, [128, 512], F32, es_l)))
            pend = {}

            def front(bi):
                qs, nq = blocks[bi]
                st = sets[bi % 2]
                kk = lambda n: ('ln', n, bi % 2)
                yk = ('Yb', qs)
                ybf, ysq, m2, rs, nm = st['ybf'], st['ysq'], st['m2'], st['rs'], st['nm']
                ws = slice(qs - yoff, qs - yoff + nq)
                EVAC(ybf[:, :, 0:nq], Y[:, :, ws], r=[yk], w=[kk('ybf')], eng='dve')
                ACT(ysq[:, :, 0:nq], Y[:, :, ws], AF.Square, r=[yk], w=[kk('ysq')])
                p1, p1k = PS()
                p2, p2k = PS()
                for c in range(NCH):
                    MM(p1[:, 0:nq], ones_d[:], ybf[:, c, 0:nq], start=(c == 0), stop=(c == NCH - 1), r=[kk('ybf'), 'ones_d'], w=[p1k])
                for c in range(NCH):
                    MM(p2[:, 0:nq], ones_d[:], ysq[:, c, 0:nq], start=(c == 0), stop=(c == NCH - 1), r=[kk('ysq'), 'ones_d'], w=[p2k])
                pend[bi] = (p1, p1k, p2, p2k)

            def front_b(bi):
                qs, nq = blocks[bi]
                st = sets[bi % 2]
                kk = lambda n: ('ln', n, bi % 2)
                m2, rs, nm = st['m2'], st['rs'], st['nm']
                p1, p1k, p2, p2k = pend[bi]
                ACT(m2[:, 0:nq], p1[:, 0:nq], AF.Square, r=[p1k], w=[kk('m2')])
                TT('dve', rs[:, 0:nq], p2[:, 0:nq], m2[:, 0:nq], ALU.subtract, r=[p2k, kk('m2')], w=[kk('rs')])
                ACT(rs[:, 0:nq], rs[:, 0:nq], AF.Ln, r=[kk('rs')], w=[kk('rs')], bias=LN_EPS)
                ACT(rs[:, 0:nq], rs[:, 0:nq], AF.Exp, r=[kk('rs')], w=[kk('rs')], scale=-0.5)
                STT('dve', nm[:, 0:nq], p1[:, 0:nq], -1.0, rs[:, 0:nq], ALU.mult, ALU.mult, r=[p1k, kk('rs')], w=[kk('nm')])

            def back(bi):
                qs, nq = blocks[bi]
                st = sets[bi % 2]
                kk = lambda n: ('ln', n, bi % 2)
                yk = ('Yb', qs)
                rs, nm = st['rs'], st['nm']
                ws = slice(qs - yoff, qs - yoff + nq)
                TT('dve', Y[:, :, ws], Y[:, :, ws], rs[:, 0:nq].unsqueeze(1).broadcast_to([128, NCH, nq]), ALU.mult,
                   r=[yk, kk('rs')], w=[yk])
                TT('dve', Y[:, :, ws], Y[:, :, ws], nm[:, 0:nq].unsqueeze(1).broadcast_to([128, NCH, nq]), ALU.add,
                   r=[yk, kk('nm')], w=[yk])

            def back_b(bi):
                qs, nq = blocks[bi]
                yk = ('Yb', qs)
                ws = slice(qs - yoff, qs - yoff + nq)
                for c in range(NCH):
                    ACT(Y[:, c, ws], Y[:, c, ws], AF.Identity, r=[yk, 'pp'], w=[yk],
                        bias=pp[:, bcol + c:bcol + c + 1], scale=pp[:, gcol + c:gcol + c + 1])
                if Ybf is not None:
                    EVAC(Ybf[:, 0:4, ws], Y[:, 0:4, ws], r=[yk], w=[('Ybfb', qs)], eng='dve')
                    EVAC(Ybf[:, 4:8, ws], Y[:, 4:8, ws], r=[yk], w=[('Ybfb2', qs)], eng='dve')

            front(0)
            front_b(0)
            for bi in range(len(blocks)):
                if bi + 1 < len(blocks):
                    front(bi + 1)
                back(bi)
                if bi + 1 < len(blocks):
                    front_b(bi + 1)
                back_b(bi)
                if post_block is not None:
                    post_block(bi)

        def proj_residual(wsrc, A, akeys, nk, Y, res_fn, es_p, tag, blocks, yoff):
            WS = [T("ws_%s%d" % (tag, i), [128, nk, 128], BF16, es_p) for i in range(3)]
            for dm in range(NCH):
                wsl = WS[dm % 3]
                wk_ = ('ws', tag, dm % 3)
                DMA('pool', wsl[:], wsrc[:, dm * 128:(dm + 1) * 128].rearrange("(c p) n -> p c n", p=128), wk_, w=[wk_])
                for (qs, nq) in blocks:
                    p, pk = PS()
                    for c in range(nk):
                        MM(p[:, 0:nq], wsl[:, c, :], A[:, c, qs - yoff:qs - yoff + nq], start=(c == 0), stop=(c == nk - 1),
                           r=[wk_] + (akeys(qs) if callable(akeys) else akeys), w=[pk])
                    res_ap, rkeys = res_fn(dm, qs, nq)
                    STT('dve', Y[:, dm, qs - yoff:qs - yoff + nq], res_ap, ALPHA, p[:, 0:nq], ALU.mult, ALU.add,
                        r=[pk] + rkeys, w=[('Yb', qs)])

        def dump(name, t):
            if not debug:
                return
            with ExitStack() as es_d:
                tmp = T("dtmp_" + name, [128, TP], F32, es_d)
                for c in range(NCH):
                    EVAC(tmp[:], t[:, c, :], r=['tmpd_dma'], w=['tmpd'], eng='dve')
                    DMA('sp', dbg[name][c * 128:(c + 1) * 128, :], tmp[:], 'dbg', r=['tmpd'], w=['tmpd_dma'])
                S.flush(final_waits=['dbg'])

        psmod[0] = 7
        with ExitStack() as es_b:
            Ybf = T("Ybf", [128, NCH, TP], BF16, es_b)
            with ExitStack() as es_y:
                Y = T("Y", [128, NCH, TP], F32, es_y)
                A = T("A", [128, NCH, TP], BF16, es_y)
                with ExitStack() as es_w:
                    for (qs_, nq_) in QB:
                        DMA('sp', A[:, :, qs_:qs_ + nq_], mixs[:, qs_:qs_ + nq_].rearrange("(c p) n -> p c n", p=128), 'mixl',
                            w=[('A', qs_)])
                    XR = [T("xr%d" % i, [128, 512], F32, es_w) for i in range(3)]
                    nxr = [0]

                    def res_x(dm, qs, nq):
                        i = nxr[0] % 3
                        nxr[0] += 1
                        k = ('xr', i)
                        DMA('sp', XR[i][:, 0:nq], xT[dm * 128:(dm + 1) * 128, P0 + qs:P0 + qs + nq], k, w=[k])
                        return XR[i][:, 0:nq], [k]
                    proj_residual(w_out, A, (lambda q_: [('A', q_)]), NCH, Y, res_x, es_w, "wo", QB, 0)
                    if debug:
                        S.flush()
                        dump("d_mixed", A)
                    layer_norm(Y, Ybf, PP_LN + 0, PP_LN + 8, es_w, QB, 0)
                    S.flush()
                dump("d_x1", Y)

                with ExitStack() as es_x:
                    OX = A
                    memb = T("memb", [128, NCH, 256], BF16, es_x)
                    KmT = T("KmT", [128, NCH, 256], BF16, es_x)
                    Vm = T("Vm", [128, 2, D], BF16, es_x)
                    QX = [T("QX%d" % i, [128, 2, TP], BF16, es_x) for i in range(2)]
                    PX = [T("PX%d" % i, [128, 512], BF16, es_x) for i in range(4)]
                    rden = T("rden", [128, 512], F32, es_x)
                    es_x3 = ExitStack()
                    WSQ = [T("wsq%d" % i, [128, NCH, 128], BF16, es_x3) for i in range(3)]
                    with ExitStack() as es_x1:
                        WSK = [T("wsk%d" % i, [128, NCH, 128], BF16, es_x1) for i in range(2)]
                        WVV = T("wvv", [128, NCH, 512], BF16, es_x1)
                        for c in range(NCH):
                            DMA('pool', memb[:, c, :], memT[c * 128:(c + 1) * 128, :], 'memb', w=['memb'])
                        for hc in range(NCH):
                            wsl = WSK[hc % 2]
                            k_ = ('wsk', hc % 2)
                            DMA('pool', wsl[:], slab(xa_wkv, hc * 128, 128), k_, w=[k_])
                            p, pk = PS()
                            for c in range(NCH):
                                MM(p[:, 0:256], wsl[:, c, :], memb[:, c, :], start=(c == 0), stop=(c == NCH - 1), r=[k_, 'memb'], w=[pk])
                            EVAC(KmT[:, hc, :], p[:, 0:256], r=[pk], w=['KmT'])
                        for hv in range(2):
                            DMA('pool', WVV[:], slab(xa_wkv, D + hv * 512, 512), 'wvv', w=['wvv'])
                            for mt in range(2):
                                p, pk = PS()
                                for c in range(NCH):
                                    MM(p[:, :], memb[:, c, mt * 128:(mt + 1) * 128], WVV[:, c, :], start=(c == 0), stop=(c == NCH - 1),
                                       r=['wvv', 'memb'], w=[pk])
                                EVAC(Vm[:, mt, hv * 512:(hv + 1) * 512], p[:, :], r=[pk], w=['Vm'])
                        es_x1_keep = es_x1.pop_all()
                    with ExitStack() as es_x2:
                        nws = 0
                        npx = 0
                        for h in range(4):
                            qx = QX[h % 2]
                            qk_ = ('QX', h % 2)
                            for dc in range(2):
                                wsl = WSQ[nws % 3]
                                k_ = ('wsq', nws % 3)
                                nws += 1
                                DMA('pool', wsl[:], slab(xa_wq, (2 * h + dc) * 128, 128), k_, w=[k_])
                                for (qs, nq) in QB:
                                    p, pk = PS()
                                    for c in range(NCH):
                                        MM(p[:, 0:nq], wsl[:, c, :], Ybf[:, c, qs:qs + nq], start=(c == 0), stop=(c == NCH - 1),
                                           r=[k_, ('Ybfb', qs), ('Ybfb2', qs)], w=[pk])
                                    EVAC(qx[:, dc, qs:qs + nq], p[:, 0:nq], r=[pk], w=[qk_])
                            for (qs, nq) in QB:
                                pxs = []
                                for mt in range(2):
                                    p, pk = PS()
                                    for dc in range(2):
                                        MM(p[:, 0:nq], KmT[:, 2 * h + dc, mt * 128:(mt + 1) * 128], qx[:, dc, qs:qs + nq],
                                           start=(dc == 0), stop=(dc == 1), r=['KmT', qk_], w=[pk])
                                    px = PX[npx % 4]
                                    pxk = ('PX', npx % 4)
                                    npx += 1
                                    ACT(px[:, 0:nq], p[:, 0:nq], AF.Exp, r=[pk], w=[pxk], scale=1.0 / 16.0)
                                    pxs.append((px, pxk))
                                pd, pdk = PS()
                                for mt in range(2):
                                    MM(pd[:, 0:nq], ones_1[:], pxs[mt][0][:, 0:nq], start=(mt == 0), stop=(mt == 1),
                                       r=[pxs[mt][1], 'ones_1'], w=[pdk])
                                ACT(rden[:, 0:nq], pd[:, 0:nq], AF.Ln, r=[pdk], w=['rden'])
                                ACT(rden[:, 0:nq], rden[:, 0:nq], AF.Exp, r=['rden'], w=['rden'], scale=-1.0)
                                for dvc in range(2):
                                    p, pk = PS()
                                    for mt in range(2):
                                        MM(p[:, 0:nq], Vm[:, mt, (2 * h + dvc) * 128:(2 * h + dvc + 1) * 128], pxs[mt][0][:, 0:nq],
                                           start=(mt == 0), stop=(mt == 1), r=[pxs[mt][1], 'Vm'], w=[pk])
                                    TT('dve', OX[:, 2 * h + dvc, qs:qs + nq], p[:, 0:nq], rden[:, 0:nq], ALU.mult,
                                       r=[pk, 'rden'], w=['OX'])

                        S.flush()
                        es_x1_keep.close()
                        es_x3.close()
                        es_x.close()

                        def res_y(dm, qs, nq):
                            return Y[:, dm, qs:qs + nq], [('Yb', qs)]
                        proj_residual(xa_wo, OX, ['OX'], NCH, Y, res_y, es_x2, "xo", QB, 0)
                        def spill_block(bi):
                            qs, nq = QB[bi]
                            for c in range(NCH):
                                DMA('sp', x2s[c * 128:(c + 1) * 128, qs:qs + nq], Y[:, c, qs:qs + nq], 'x2s', r=[('Yb', qs)], w=[('x2s', c, bi)])
                        layer_norm(Y, Ybf, PP_LN + 16, PP_LN + 24, es_x2, QB, 0, post_block=spill_block)
                        S.flush(final_waits=['x2s'])
                dump("d_x2", Y)

            HT = TP // 2
            WU = [T("wu%d" % i, [128, NCH, 128], BF16, es_b) for i in range(4)]
            WD = [T("wd%d" % i, [128, NFT, 128], BF16, es_b) for i in range(2)]

            def wd_dma(dm):
                wk_ = ('wd', dm % 2)
                DMA('pool', WD[dm % 2][:], w_down[:, dm * 128:(dm + 1) * 128].rearrange("(c p) n -> p c n", p=128), wk_, w=[wk_])

            def wu_dma(nw):
                i_, br_ = nw // 2, nw % 2
                k_ = ('wu', nw % 4)
                DMA('pool', WU[nw % 4][:], slab(w_up, (i_ + br_ * NFT) * 128, 128), k_, w=[k_])
            for hh in range(2):
                t0 = hh * HT
                lb_ = 0 if hh == 0 else 2
                ublocks = []
                s_ = t0 - lb_
                while s_ < t0 + HT:
                    n_ = min(512, t0 + HT - s_)
                    ublocks.append((s_, n_))
                    s_ += n_
                oblocks = []
                s_ = t0
                while s_ < t0 + HT:
                    n_ = min(512, t0 + HT - s_)
                    oblocks.append((s_, n_))
                    s_ += n_
                with ExitStack() as es_f:
                    Hm = T("Hm", [128, NFT, HT], BF16, es_f)
                    with ExitStack() as es_u:
                        UA = [T("ua%d" % i, [128, HT + 2], F32, es_u) for i in range(2)]
                        UG = [T("ug%d" % i, [128, HT + 2], F32, es_u) for i in range(2)]
                        CA = [T("ca%d" % i, [128, HT], F32, es_u) for i in range(2)]
                        CG = [T("cg%d" % i, [128, HT], F32, es_u) for i in range(2)]
                        for i in range(2):
                            MEMSET('dve', UA[i][:, 0:2], 0.0, w=[('UA', i)])
                            MEMSET('dve', UG[i][:, 0:2], 0.0, w=[('UG', i)])
                        nwu = 0

                        def gate(i):
                            b2 = i % 2
                            ACT(CG[b2][:, :], CG[b2][:, :], AF.Silu, r=[('CG', b2)], w=[('CG', b2)])
                            TT('dve', Hm[:, i, :], CG[b2][:, :], CA[b2][:, :], ALU.mult, r=[('CG', b2), ('CA', b2)], w=['Hm'])
                        for i in range(NFT):
                            b2 = i % 2
                            for br, (U, uk, C, ck, ceng) in enumerate(((UA[b2], ('UA', b2), CA[b2], ('CA', b2), 'dve'),
                                                                       (UG[b2], ('UG', b2), CG[b2], ('CG', b2), 'dve'))):
                                ft = i + br * NFT
                                wsl = WU[nwu % 4]
                                k_ = ('wu', nwu % 4)
                                if not (hh == 1 and nwu < 4):
                                    wu_dma(nwu)
                                nwu += 1
                                for (us, un) in ublocks:
                                    p, pk = PS()
                                    for c in range(NCH):
                                        MM(p[:, 0:un], wsl[:, c, :], Ybf[:, c, us:us + un], start=(c == 0), stop=(c == NCH - 1),
                                           r=[k_, 'Ybf'], w=[pk])
                                    uc = us - t0 + 2
                                    EVAC(U[:, uc:uc + un], p[:, 0:un], r=[pk], w=[uk], eng='act')
                                if hh == 0:
                                    TS(ceng, U[:, 2:130], U[:, 2:130], pp[:, PP_HALO:PP_HALO + 1], None, ALU.mult, r=[uk, 'pp'], w=[uk])
                                cw = PP_CW + ft * 3
                                cb = PP_CB + ft
                                TS(ceng, C[:, :], U[:, 2:HT + 2], pp[:, cw + 2:cw + 3], pp[:, cb:cb + 1], ALU.mult, ALU.add,
                                   r=[uk, 'pp'], w=[ck])
                                STT(ceng, C[:, :], U[:, 1:HT + 1], pp[:, cw + 1:cw + 2], C[:, :], ALU.mult, ALU.add, r=[uk, ck, 'pp'], w=[ck])
                                STT(ceng, C[:, :], U[:, 0:HT], pp[:, cw:cw + 1], C[:, :], ALU.mult, ALU.add, r=[uk, ck, 'pp'], w=[ck])
                            if i >= 1:
                                gate(i - 1)
                        gate(NFT - 1)
                        wd_dma(0)
                        wd_dma(1)
                        S.flush()
                    with ExitStack() as es_d:
                        Y3 = T("Y3", [128, NCH, HT], F32, es_d)
                        XR = [T("xr3_%d" % i, [128, 512], F32, es_d) for i in range(3)]
                        nxr3 = [0]

                        def res_x2(dm, qs, nq):
                            i = nxr3[0] % 3
                            nxr3[0] += 1
                            k = ('xr3', i)
                            DMA('sp', XR[i][:, 0:nq], x2s[dm * 128:(dm + 1) * 128, qs:qs + nq], k, w=[k])
                            return XR[i][:, 0:nq], [k]
                        for dm in range(NCH):
                            wsl = WD[dm % 2]
                            wk_ = ('wd', dm % 2)
                            if dm >= 2:
                                wd_dma(dm)
                            if dm == NCH - 1 and hh == 0:
                                for nw_ in range(4):
                                    wu_dma(nw_)
                            for (qs, nq) in oblocks:
                                p, pk = PS()
                                for c in range(NFT):
                                    MM(p[:, 0:nq], wsl[:, c, :], Hm[:, c, qs - t0:qs - t0 + nq], start=(c == 0), stop=(c == NFT - 1),
                                       r=[wk_, 'Hm'], w=[pk])
                                res_ap, rkeys = res_x2(dm, qs, nq)
                                STT('dve', Y3[:, dm, qs - t0:qs - t0 + nq], res_ap, ALPHA, p[:, 0:nq], ALU.mult, ALU.add,
                                    r=[pk] + rkeys, w=[('Yb', qs)])
                        def store_block(bi, t0=t0, oblocks=oblocks, Y3=Y3):
                            qs, nq = oblocks[bi]
                            lo = max(qs, 128)
                            if lo >= qs + nq:
                                return
                            for c in range(NCH):
                                DMA('sp', outT[c * 128:(c + 1) * 128, lo - 128:qs + nq - 128], Y3[:, c, lo - t0:qs + nq - t0], 'out',
                                    r=[('Yb', qs)], w=[('outd', c, bi)])
                        layer_norm(Y3, None, PP_LN + 32, PP_LN + 40, es_d, oblocks, t0, post_block=store_block)
                        S.flush(final_waits=['out'], last=(hh == 1))
    return nc


def _host_inputs(x, mem, w_in, b_fox_f, hgrn_lb_logits, hgrn_norm_g, w_out, ln1_g, ln1_b, xa_wq, xa_wkv, xa_wo,
                 ln2_g, ln2_b, ffn_w_up, ffn_conv_w, ffn_conv_b, ffn_w_down, ln3_g, ln3_b):
    f = lambda a: np.ascontiguousarray(np.asarray(a, dtype=np.float32))
    x = f(x); mem = f(mem); w_in = f(w_in)[0]
    wqp = np.zeros((8, D, 72), np.float32)
    wkp = np.zeros((8, D, 72), np.float32)
    for h in range(8):
        wqp[h, :, 0:64] = w_in[:, h * 64:(h + 1) * 64]
        wkp[h, :, 0:64] = w_in[:, 512 + h * 64:512 + (h + 1) * 64]
    wffp = np.zeros((D, 128), np.float32)
    for h in range(8):
        for g in range(3):
            wffp[:, 32 * g + h] = w_in[:, 1536 + h]
    pp = np.zeros((128, PP_N), np.float32)
    for i, v in enumerate((ln1_g, ln1_b, ln2_g, ln2_b, ln3_g, ln3_b)):
        pp[:, PP_LN + 8 * i:PP_LN + 8 * i + 8] = f(v)[0].reshape(8, 128).T
    lbl = f(hgrn_lb_logits)
    for l in range(2):
        pp[:, PP_LBL + 4 * l:PP_LBL + 4 * l + 4] = lbl[l].reshape(4, 128).T
    pp[:, PP_NG:PP_NG + 4] = f(hgrn_norm_g)[0].reshape(4, 128).T
    bf = f(b_fox_f)[0]
    for h in range(8):
        for g in range(3):
            pp[32 * g + h, PP_BFF] = bf[h]
    pp[:, PP_CB:PP_CB + 44] = f(ffn_conv_b)[0].reshape(44, 128).T
    cw = f(ffn_conv_w)[0]
    for k in range(3):
        pp[:, PP_CW + k:PP_CW + 132:3] = cw[k].reshape(44, 128).T
    cst = np.zeros((128, C_N), np.float32)
    idx = np.arange(128)
    cst[:, C_TRI:C_TRI + 128] = (idx[:, None] <= idx[None, :])
    cst[:, C_HM:C_HM + 128] = (idx[:, None] <= idx[None, :]) & ((idx[:, None] // 64) == (idx[None, :] // 64))
    cst[:, C_ID:C_ID + 128] = np.eye(128)
    cst[64, C_SBE:C_SBE + 64] = 1.0
    cst[0, C_SBO + 64:C_SBO + 128] = 1.0
    for h in range(8):
        q0 = C_SQ + h * 72
        k0 = C_SK + h * 72
        for g in range(3):
            cst[32 * g + h, q0 + 64 + g] = 1.0
            cst[32 * g + h, k0 + 67 + g] = 1.0
        cst[96, q0 + 67:q0 + 71] = 1.0
        cst[96, k0 + 64:k0 + 67] = 1.0
        cst[97, k0 + 70] = 1.0
    shared = dict(wqp=wqp, wkp=wkp, wffp=wffp, w_in=w_in, w_out=f(w_out)[0], xa_wq=f(xa_wq)[0], xa_wkv=f(xa_wkv)[0],
                  xa_wo=f(xa_wo)[0], w_up=f(ffn_w_up)[0], w_down=f(ffn_w_down)[0], cst=cst)
    in_maps = []
    for core in range(8):
        b, s = core // 2, core % 2
        xt = np.zeros((D, TCTX), np.float32)
        aux = np.zeros((3, TCTX), np.float32)
        aux[0] = 1.0
        aux[2] = 8.0
        if s == 0:
            xt[:, 2048:] = x[b, 0:2048].T
            aux[1, 0:2048] = -30000.0
        else:
            xt[:, :] = x[b].T
        ppc = pp.copy()
        ppc[:, PP_HALO] = float(s)
        m = dict(shared)
        m.update(xT=xt, memT=np.ascontiguousarray(mem[b].T), aux=aux, pp=ppc)
        in_maps.append(m)
    return in_maps


def kernel(**inputs):
    in_maps = _host_inputs(**inputs)
    nc = build_program(debug=False)
    res = run_bass_kernel_spmd(nc, in_maps, core_ids=list(range(8)))
    out = np.empty((4, 4096, D), np.float32)
    for core in range(8):
        b, s = core // 2, core % 2
        out[b, s * 2048:(s + 1) * 2048, :] = np.asarray(res.results[core]["outT"]).T
    return out
```
